# Optimizing a Trainium2 kernel written in Bass

```python
import math
import jax, jax.numpy as jnp
from jax import lax
import numpy as np

D_MODEL = 2048
BATCH = 1
SEQ = 8192
DEPTH = 2

HEAD_DIM = 128
N_HEADS_DIFF = 4
DIFF_QK_DIM = 64
DIFF_V_DIM = 2 * DIFF_QK_DIM
DIFF_LAMBDA_STD = 0.1
N_HEADS_FOX = 6
FOX_HEAD_DIM = HEAD_DIM
FORGET_BIAS_MIN = 1.0
FORGET_BIAS_MAX = 4.0
N_HEADS_MLA = 6
MLA_Q_RANK = 512
MLA_KV_RANK = 256
MLA_NOPE_DIM = 128
MLA_ROPE_DIM = 64
MLA_QK_DIM = MLA_ROPE_DIM + MLA_NOPE_DIM
MLA_V_DIM = 128
MIX_WIDTH = N_HEADS_DIFF * DIFF_V_DIM + N_HEADS_FOX * FOX_HEAD_DIM + N_HEADS_MLA * MLA_V_DIM
D_FF = 4 * D_MODEL
ROPE_THETA = 500000.0
PARTIAL_ROT_DIM = DIFF_QK_DIM // 4
BLOCK_Q = 128
EPS = 1e-6

DIFF_Q_COLS = N_HEADS_DIFF * 2 * DIFF_QK_DIM
DIFF_V_COLS = N_HEADS_DIFF * DIFF_V_DIM
FOX_COLS = N_HEADS_FOX * FOX_HEAD_DIM
MLA_KV_DOWN_COLS = MLA_KV_RANK + MLA_ROPE_DIM
IN_SECTIONS = (DIFF_Q_COLS, DIFF_Q_COLS, DIFF_V_COLS,
               FOX_COLS, FOX_COLS, FOX_COLS, N_HEADS_FOX,
               MLA_Q_RANK, MLA_KV_DOWN_COLS)
IN_COLS = 2 * DIFF_Q_COLS + DIFF_V_COLS + 3 * FOX_COLS + N_HEADS_FOX + MLA_Q_RANK + MLA_KV_DOWN_COLS

kernel_name = "hymba_style_diff_fox_mla_trunk"


def rms_norm(x, g):
    xf = x.astype(jnp.float32)
    y = xf * lax.rsqrt(jnp.mean(xf * xf, axis=-1, keepdims=True) + EPS)
    return (y * g.astype(jnp.float32)).astype(x.dtype)


def rope_tables(seq_len, rot_dim):
    half = rot_dim // 2
    inv_freq = ROPE_THETA ** (-jnp.arange(half, dtype=jnp.float32) / half)
    ang = jnp.arange(seq_len, dtype=jnp.float32)[:, None] * inv_freq[None, :]
    return jnp.cos(ang), jnp.sin(ang)


def apply_rope(x, cos, sin):
    half = cos.shape[-1]
    rot_dim = 2 * half
    c = cos[None, :, None, :]
    s = sin[None, :, None, :]
    xr = x[..., :rot_dim].astype(jnp.float32)
    x1, x2 = xr[..., :half], xr[..., half:]
    rotated = jnp.concatenate([x1 * c - x2 * s, x2 * c + x1 * s], axis=-1).astype(x.dtype)
    return jnp.concatenate([rotated, x[..., rot_dim:]], axis=-1)


def causal_block_attention(q, k, v, scale, log_decay_cum=None):
    b, s, h, dk = q.shape
    nb = s // BLOCK_Q
    q_blocks = jnp.moveaxis(q.reshape(b, nb, BLOCK_Q, h, dk), 1, 0)
    if log_decay_cum is None:
        c_blocks, c_keys = None, None
    else:
        c_blocks = jnp.moveaxis(log_decay_cum.reshape(b, nb, BLOCK_Q, h), 1, 0)
        c_keys = jnp.transpose(log_decay_cum, (0, 2, 1))[:, :, None, :]
    k_pos = jnp.arange(s)

    def attend(args):
        i, qi, ci = args
        logits = jnp.einsum("bqhd,bkhd->bhqk", qi, k, preferred_element_type=jnp.float32) * scale
        if ci is not None:
            logits = logits + jnp.transpose(ci, (0, 2, 1))[..., None] - c_keys
        q_pos = i * BLOCK_Q + jnp.arange(BLOCK_Q)
        logits = jnp.where(k_pos[None, :] <= q_pos[:, None], logits, -jnp.inf)
        p = jax.nn.softmax(logits, axis=-1).astype(v.dtype)
        return jnp.einsum("bhqk,bkhv->bqhv", p, v)

    out = lax.map(attend, (jnp.arange(nb), q_blocks, c_blocks))
    return jnp.moveaxis(out, 0, 1).reshape(b, s, h, v.shape[-1])


def diff_attention(q, k, v, cos, sin, q_norm, k_norm, lq1, lk1, lq2, lk2, subln, layer_idx):
    b, s, _ = q.shape
    q = rms_norm(q.reshape(b, s, 2 * N_HEADS_DIFF, DIFF_QK_DIM), q_norm)
    k = rms_norm(k.reshape(b, s, 2 * N_HEADS_DIFF, DIFF_QK_DIM), k_norm)
    q = apply_rope(q, cos, sin).reshape(b, s, N_HEADS_DIFF, 2, DIFF_QK_DIM)
    k = apply_rope(k, cos, sin).reshape(b, s, N_HEADS_DIFF, 2, DIFF_QK_DIM)
    v = v.reshape(b, s, N_HEADS_DIFF, DIFF_V_DIM)
    lambda_init = 0.8 - 0.6 * math.exp(-0.3 * layer_idx)
    lam = (jnp.exp(jnp.sum(lq1.astype(jnp.float32) * lk1.astype(jnp.float32)))
           - jnp.exp(jnp.sum(lq2.astype(jnp.float32) * lk2.astype(jnp.float32)))
           + lambda_init)
    scale = DIFF_QK_DIM ** -0.5
    o1 = causal_block_attention(q[:, :, :, 0], k[:, :, :, 0], v, scale)
    o2 = causal_block_attention(q[:, :, :, 1], k[:, :, :, 1], v, scale)
    o = o1.astype(jnp.float32) - lam * o2.astype(jnp.float32)
    o = rms_norm(o, subln) * (1.0 - lambda_init)
    return o.reshape(b, s, N_HEADS_DIFF * DIFF_V_DIM).astype(v.dtype)


def forgetting_attention(q, k, v, f_logit, f_bias, q_norm, k_norm):
    b, s, _ = q.shape
    q = rms_norm(q.reshape(b, s, N_HEADS_FOX, FOX_HEAD_DIM), q_norm)
    k = rms_norm(k.reshape(b, s, N_HEADS_FOX, FOX_HEAD_DIM), k_norm)
    v = v.reshape(b, s, N_HEADS_FOX, FOX_HEAD_DIM)
    log_f = jax.nn.log_sigmoid(f_logit.astype(jnp.float32) + f_bias.astype(jnp.float32))
    cum = jnp.cumsum(log_f, axis=1)
    o = causal_block_attention(q, k, v, FOX_HEAD_DIM ** -0.5, cum)
    return o.reshape(b, s, N_HEADS_FOX * FOX_HEAD_DIM)


def latent_attention(q_down, kv_down, cos, sin, q_a_norm, kv_a_norm, w_q_up, w_kv_up, q_norm, k_norm):
    b, s, _ = q_down.shape
    q = (rms_norm(q_down, q_a_norm) @ w_q_up).reshape(b, s, N_HEADS_MLA, MLA_QK_DIM)
    c_kv = rms_norm(kv_down[..., :MLA_KV_RANK], kv_a_norm)
    k_rope = kv_down[..., MLA_KV_RANK:].reshape(b, s, 1, MLA_ROPE_DIM)
    kv = (c_kv @ w_kv_up).reshape(b, s, N_HEADS_MLA, MLA_NOPE_DIM + MLA_V_DIM)
    k_nope, v = kv[..., :MLA_NOPE_DIM], kv[..., MLA_NOPE_DIM:]
    q_rope = apply_rope(rms_norm(q[..., :MLA_ROPE_DIM], q_norm[:MLA_ROPE_DIM]), cos, sin)
    q_nope = rms_norm(q[..., MLA_ROPE_DIM:], q_norm[MLA_ROPE_DIM:])
    k_rope = apply_rope(rms_norm(k_rope, k_norm[:MLA_ROPE_DIM]), cos, sin)
    k_nope = rms_norm(k_nope, k_norm[MLA_ROPE_DIM:])
    q = jnp.concatenate([q_rope, q_nope], axis=-1)
    k = jnp.concatenate([jnp.broadcast_to(k_rope, (b, s, N_HEADS_MLA, MLA_ROPE_DIM)), k_nope], axis=-1)
    o = causal_block_attention(q, k, v, MLA_QK_DIM ** -0.5)
    return o.reshape(b, s, N_HEADS_MLA * MLA_V_DIM)


def split_columns(proj):
    parts, start = [], 0
    for width in IN_SECTIONS:
        parts.append(proj[..., start:start + width])
        start += width
    return parts


def setup_inputs(seed: int = 0) -> dict:
    key = jax.random.key(seed)
    ks = jax.random.split(key, 24)

    def normal(k, shape, scale):
        return scale * jax.random.normal(k, shape, jnp.float32)

    def gain(k, dim):
        return 1.0 + normal(k, (DEPTH, dim), 0.05)

    L = DEPTH
    return {
        "x": normal(ks[0], (BATCH, SEQ, D_MODEL), 1.0),
        "norm_mix": gain(ks[1], D_MODEL),
        "w_in": normal(ks[2], (L, D_MODEL, IN_COLS), D_MODEL ** -0.5),
        "diff_q_norm": gain(ks[3], DIFF_QK_DIM),
        "diff_k_norm": gain(ks[4], DIFF_QK_DIM),
        "diff_lambda_q1": normal(ks[5], (L, DIFF_QK_DIM), DIFF_LAMBDA_STD),
        "diff_lambda_k1": normal(ks[6], (L, DIFF_QK_DIM), DIFF_LAMBDA_STD),
        "diff_lambda_q2": normal(ks[7], (L, DIFF_QK_DIM), DIFF_LAMBDA_STD),
        "diff_lambda_k2": normal(ks[8], (L, DIFF_QK_DIM), DIFF_LAMBDA_STD),
        "diff_subln": gain(ks[9], DIFF_V_DIM),
        "fox_q_norm": gain(ks[10], FOX_HEAD_DIM),
        "fox_k_norm": gain(ks[11], FOX_HEAD_DIM),
        "fox_forget_bias": jax.random.uniform(ks[12], (L, N_HEADS_FOX), jnp.float32,
                                              FORGET_BIAS_MIN, FORGET_BIAS_MAX),
        "mla_q_a_norm": gain(ks[13], MLA_Q_RANK),
        "mla_kv_a_norm": gain(ks[14], MLA_KV_RANK),
        "mla_w_q_up": normal(ks[15], (L, MLA_Q_RANK, N_HEADS_MLA * MLA_QK_DIM), MLA_Q_RANK ** -0.5),
        "mla_w_kv_up": normal(ks[16], (L, MLA_KV_RANK, N_HEADS_MLA * (MLA_NOPE_DIM + MLA_V_DIM)),
                              MLA_KV_RANK ** -0.5),
        "mla_q_norm": gain(ks[17], MLA_QK_DIM),
        "mla_k_norm": gain(ks[18], MLA_QK_DIM),
        "w_out": normal(ks[19], (L, MIX_WIDTH, D_MODEL), MIX_WIDTH ** -0.5),
        "norm_ffn": gain(ks[20], D_MODEL),
        "w_ff_up": normal(ks[21], (L, D_MODEL, D_FF), D_MODEL ** -0.5),
        "w_ff_down": normal(ks[22], (L, D_FF, D_MODEL), D_FF ** -0.5),
    }


def reference(x, norm_mix, w_in, diff_q_norm, diff_k_norm, diff_lambda_q1, diff_lambda_k1,
              diff_lambda_q2, diff_lambda_k2, diff_subln, fox_q_norm, fox_k_norm, fox_forget_bias,
              mla_q_a_norm, mla_kv_a_norm, mla_w_q_up, mla_w_kv_up, mla_q_norm, mla_k_norm,
              w_out, norm_ffn, w_ff_up, w_ff_down):
    seq = x.shape[1]
    cos_p, sin_p = rope_tables(seq, PARTIAL_ROT_DIM)
    cos_m, sin_m = rope_tables(seq, MLA_ROPE_DIM)
    for l in range(DEPTH):
        h = rms_norm(x, norm_mix[l])
        proj = h @ w_in[l]
        dq, dk, dv, fq, fk, fv, ff, mq, mkv = split_columns(proj)
        o_a = diff_attention(dq, dk, dv, cos_p, sin_p, diff_q_norm[l], diff_k_norm[l],
                             diff_lambda_q1[l], diff_lambda_k1[l], diff_lambda_q2[l],
                             diff_lambda_k2[l], diff_subln[l], l)
        o_b = forgetting_attention(fq, fk, fv, ff, fox_forget_bias[l], fox_q_norm[l], fox_k_norm[l])
        o_c = latent_attention(mq, mkv, cos_m, sin_m, mla_q_a_norm[l], mla_kv_a_norm[l],
                               mla_w_q_up[l], mla_w_kv_up[l], mla_q_norm[l], mla_k_norm[l])
        mixed = jnp.concatenate([o_a, o_b.astype(o_a.dtype), o_c.astype(o_a.dtype)], axis=-1)
        x = x + (mixed @ w_out[l]).astype(x.dtype)
        h = rms_norm(x, norm_ffn[l])
        x = x + (jnp.square(jax.nn.relu(h @ w_ff_up[l])) @ w_ff_down[l]).astype(x.dtype)
    return x
```

```python
import math
from contextlib import ExitStack

import numpy as np
import ml_dtypes
import concourse.bass as bass
import concourse.mybir as mybir
from concourse.bass_utils import run_bass_kernel_spmd

F32 = mybir.dt.float32
BF16 = mybir.dt.bfloat16
AF = mybir.ActivationFunctionType
ALU = mybir.AluOpType
AX = mybir.AxisListType

NCORES = 8
D = 2048
S = 8192
NT = 1024
NB = 8
DFF = 8192
IN_COLS = 4678
EPS = 1e-6
NQT = 19
NKT = 17
NH = 16
NEG = -30000.0


class Buf:
    __slots__ = ("name", "w", "r", "x")

    def __init__(self, name, excl=False):
        self.name = name
        self.w = {}
        self.r = {}
        self.x = {} if excl else None


class Ctx:
    def __init__(self, nc, es):
        self.nc = nc
        self.es = es
        self.E = {"pe": nc.tensor, "act": nc.scalar, "dve": nc.vector, "pool": nc.gpsimd, "sp": nc.sync}
        self.sem = {k: es.enter_context(nc.semaphore("s_" + k)) for k in self.E}
        self.cnt = {k: 0 for k in self.E}
        self.seen = {k: {} for k in self.E}
        self.dsem = {}
        self.nbuf = 0
        self.scopes = []

    def buf(self, name=None):
        self.nbuf += 1
        b = Buf(name or "b%d" % self.nbuf)
        for sc in self.scopes:
            sc.append(b)
        return b

    def scope(self):
        self.scopes.append([])

    def end_scope(self):
        bufs = self.scopes.pop()
        for e in self.E:
            self.wait_all(e, bufs)

    def dma_sem(self, key):
        if key not in self.dsem:
            self.dsem[key] = [self.es.enter_context(self.nc.semaphore("d_" + key)), 0]
        return self.dsem[key]

    def _handle(self, sk):
        return self.sem[sk] if sk in self.sem else self.dsem[sk][0]

    def _wait(self, e, toks):
        for sk, v in toks.items():
            if v <= self.seen[e].get(sk, 0):
                continue
            if sk == e and e == "pe":
                continue
            self.E[e].wait_ge(self._handle(sk), v)
            self.seen[e][sk] = v

    @staticmethod
    def _merge(d, other):
        for sk, v in other.items():
            if v > d.get(sk, 0):
                d[sk] = v

    def _deps(self, reads, writes, pwrites, e=None):
        toks = {}
        for b in list(reads) + list(writes) + list(pwrites):
            if b.x is not None:
                self._merge(toks, {sk: v for sk, v in b.x.items() if sk != e})
        for b in reads:
            self._merge(toks, b.w)
        for b in writes:
            self._merge(toks, b.w)
            self._merge(toks, b.r)
        for b in pwrites:
            self._merge(toks, b.r)
        return toks

    def _record(self, tok, reads, writes, pwrites):
        sk, v = tok
        for b in list(reads) + list(writes) + list(pwrites):
            if b.x is not None and v > b.x.get(sk, 0):
                b.x[sk] = v
        for b in reads:
            if v > b.r.get(sk, 0):
                b.r[sk] = v
        for b in writes:
            b.w = {sk: v}
            b.r = {}
        for b in pwrites:
            if v > b.w.get(sk, 0):
                b.w[sk] = v

    def op(self, e, fn, reads=(), writes=(), pwrites=(), inc=True):
        self._wait(e, self._deps(reads, writes, pwrites, e))
        ins = fn(self.E[e])
        if inc:
            self.cnt[e] += 1
            ins.then_inc(self.sem[e], 1)
            tok = (e, self.cnt[e])
        else:
            tok = (e, self.cnt[e] + 1)
        self._record(tok, reads, writes, pwrites)
        return tok

    def dma(self, q, key, out, in_, reads=(), writes=(), pwrites=()):
        self._wait(q, self._deps(reads, writes, pwrites, q))
        h = self.dma_sem(key)
        h[1] += 16
        self.E[q].dma_start(out=out, in_=in_).then_inc(h[0], 16)
        tok = (key, h[1])
        self._record(tok, reads, writes, pwrites)
        return tok

    def wait_all(self, e, bufs):
        toks = {}
        for b in bufs:
            self._merge(toks, b.w)
            self._merge(toks, b.r)
        self._wait(e, toks)


class Prog:
    def __init__(self, mode):
        self.mode = mode
        self.nc = bass.Bass("TRN2", target_bir_lowering=False)
        self.es = ExitStack()
        self.K = Ctx(self.nc, self.es)
        self.ins = {}
        self.outs = {}
        self.dbufs = {}

    def din(self, name, shape, dt):
        t = self.nc.dram_tensor(name, list(shape), dt, kind="ExternalInput").ap()
        self.ins[name] = t
        self.dbufs[name] = self.K.buf(name)
        return t

    def dout(self, name, shape, dt):
        t = self.nc.dram_tensor(name, list(shape), dt, kind="ExternalOutput").ap()
        self.outs[name] = t
        self.dbufs[name] = self.K.buf(name)
        return t

    def dint(self, name, shape, dt):
        t = self.nc.dram_tensor(name, list(shape), dt).ap()
        self.dbufs[name] = self.K.buf(name)
        return t

    def sb(self, stack, name, shape, dt):
        return stack.enter_context(self.nc.sbuf_tensor(name, list(shape), dt))

    def setup_common(self):
        nc, K, es = self.nc, self.K, self.es
        self.bank = [es.enter_context(nc.psum_tensor("bank%d" % i, [128, 512], F32)) for i in range(8)]
        self.bbuf = [Buf("bank%d" % i, excl=True) for i in range(8)]
        self.c_ident = self.din("c_ident", [128, 128], BF16)
        self.ident = self.sb(es, "ident", [128, 128], BF16)
        self.b_ident = K.buf("ident")
        K.dma("sp", "ident", self.ident[:], self.c_ident, writes=[self.b_ident])
        self.epsT = self.sb(es, "epsT", [128, 1], F32)
        self.b_eps = K.buf("eps")
        K.op("dve", lambda e: e.memset(self.epsT[:], EPS), writes=[self.b_eps])
        self.oneT = self.sb(es, "oneT", [128, 1], F32)
        K.op("dve", lambda e: e.memset(self.oneT[:], 1.0), pwrites=[self.b_eps])

    def load_bcast(self, stack, name, src_ap_1d, n, q="sp", key="const"):
        t = self.sb(stack, name, [128, n], F32)
        b = self.K.buf(name)
        self.K.dma(q, key, t[:], src_ap_1d.partition_broadcast(128), writes=[b])
        return t, b

    def rstd_of(self, src3, src_b, nh, dh, scr, scr_b, rs, rs_b, eng_sq="act"):
        K = self.K
        sq = scr[:, 0:nh * dh].rearrange("p (h d) -> p h d", h=nh)
        K.op("act", lambda e: e.activation(out=sq, in_=src3, func=AF.Square), reads=[src_b], writes=[scr_b])
        K.op("dve", lambda e: e.reduce_sum(out=rs[:, 0:nh], in_=sq, axis=AX.X), reads=[scr_b], writes=[rs_b])
        K.op("act", lambda e: e.activation(out=rs[:, 0:nh], in_=rs[:, 0:nh], func=AF.Sqrt,
                                           bias=self.epsT[:, 0:1], scale=1.0 / dh),
             reads=[self.b_eps], writes=[rs_b])
        K.op("dve", lambda e: e.reciprocal(out=rs[:, 0:nh], in_=rs[:, 0:nh]), writes=[rs_b])

    def norm_gain(self, dst3, dst_b, src3, src_b, rs, rs_b, gain, gain_b, nh, dh, tmp, tmp_b, pw=False):
        K = self.K
        t3 = tmp[:, 0:nh * dh].rearrange("p (h d) -> p h d", h=nh)
        K.op("dve", lambda e: e.tensor_tensor(out=t3, in0=src3, in1=rs[:, 0:nh].unsqueeze(2).broadcast_to([128, nh, dh]),
                                              op=ALU.mult), reads=[src_b, rs_b], writes=[tmp_b])
        kw = dict(pwrites=[dst_b]) if pw else dict(writes=[dst_b])
        K.op("pool", lambda e: e.tensor_tensor(out=dst3, in0=t3, in1=gain.unsqueeze(1).broadcast_to([128, nh, dh]),
                                               op=ALU.mult), reads=[tmp_b, gain_b], **kw)

    def rope(self, dst3, dst_b, src3, src_b, cos2, sin2, tab_b, nh, half, tmp, tmp_b):
        K = self.K
        c = cos2.unsqueeze(1).broadcast_to([128, nh, half])
        s = sin2.unsqueeze(1).broadcast_to([128, nh, half])
        x1 = src3[:, :, 0:half]
        x2 = src3[:, :, half:2 * half]
        t = tmp[:, 0:4 * nh * half].rearrange("p (k h d) -> p k h d", k=4, h=nh)
        K.op("dve", lambda e: e.tensor_tensor(out=t[:, 0], in0=x1, in1=c, op=ALU.mult), reads=[src_b, tab_b], writes=[tmp_b])
        K.op("dve", lambda e: e.tensor_tensor(out=t[:, 1], in0=x2, in1=s, op=ALU.mult), reads=[src_b, tab_b], pwrites=[tmp_b])
        K.op("dve", lambda e: e.tensor_tensor(out=t[:, 2], in0=x2, in1=c, op=ALU.mult), reads=[src_b, tab_b], pwrites=[tmp_b])
        K.op("dve", lambda e: e.tensor_tensor(out=t[:, 3], in0=x1, in1=s, op=ALU.mult), reads=[src_b, tab_b], pwrites=[tmp_b])
        K.op("dve", lambda e: e.tensor_tensor(out=dst3[:, :, 0:half], in0=t[:, 0], in1=t[:, 1], op=ALU.subtract),
             reads=[tmp_b], pwrites=[dst_b])
        K.op("dve", lambda e: e.tensor_tensor(out=dst3[:, :, half:2 * half], in0=t[:, 2], in1=t[:, 3], op=ALU.add),
             reads=[tmp_b], pwrites=[dst_b])

    def transpose_to(self, tm, tm_b, nchunk, dsts, dst_b, bank_i, evac="act"):
        K = self.K
        pb = self.bank[bank_i][:, :].bitcast(BF16)
        bb = self.bbuf[bank_i]
        for q in range(nchunk):
            K.op("pe", lambda e, q=q: e.transpose(out=pb[:, q * 128:(q + 1) * 128], in_=tm[:, q * 128:(q + 1) * 128],
                                                  identity=self.ident[:]),
                 reads=[tm_b, self.b_ident], writes=[bb] if q == 0 else (), pwrites=() if q == 0 else [bb],
                 inc=(q == nchunk - 1))
        src = pb[:, 0:nchunk * 128].rearrange("p (q t) -> p q t", q=nchunk)
        if evac == "act":
            K.op("act", lambda e: e.copy(out=dsts, in_=src), reads=[bb], pwrites=[dst_b])
        else:
            K.op("dve", lambda e: e.tensor_copy(out=dsts, in_=src), reads=[bb], pwrites=[dst_b])

    def rms_to_hT(self, xs, xs_b, gT, g_b, hT, hT_b, stack):
        K = self.K
        scr = self.sb(stack, "n_scr", [128, 2048], F32)
        scr_b = K.buf()
        xn = [self.sb(stack, "n_xn%d" % i, [128, 2048], BF16) for i in range(2)]
        xn_b = [K.buf() for _ in range(2)]
        rs = [self.sb(stack, "n_rs%d" % i, [128, 1], F32) for i in range(2)]
        rs_b = [K.buf() for _ in range(2)]
        for tb in range(NB):
            s = tb % 2
            K.op("act", lambda e: e.activation(out=scr[:], in_=xs[:, tb, :], func=AF.Square, accum_out=rs[s][:, 0:1]),
                 reads=[xs_b], writes=[scr_b, rs_b[s]])
            K.op("act", lambda e: e.activation(out=rs[s][:], in_=rs[s][:], func=AF.Sqrt, bias=self.epsT[:, 0:1],
                                               scale=1.0 / D), reads=[self.b_eps], writes=[rs_b[s]])
            K.op("dve", lambda e: e.reciprocal(out=rs[s][:], in_=rs[s][:]), writes=[rs_b[s]])
            K.op("dve", lambda e: e.scalar_tensor_tensor(out=xn[s][:], in0=xs[:, tb, :], scalar=rs[s][:, 0:1], in1=gT[:],
                                                         op0=ALU.mult, op1=ALU.mult),
                 reads=[xs_b, rs_b[s], g_b], writes=[xn_b[s]])
            for half in range(2):
                self.transpose_to(xn[s][:, half * 1024:(half + 1) * 1024], xn_b[s], 8,
                                  hT[:, half * 8:(half + 1) * 8, tb * 128:(tb + 1) * 128], hT_b,
                                  bank_i=6 + half, evac="act" if half == 0 else "dve")

    def wload(self, key, dst, src, dst_b, src_b=None):
        self.K.dma("pool", key, dst, src, reads=[src_b] if src_b else (), writes=[dst_b])

    def phase_A(self, L, get_xs, P, dQT, dKT, dV, dLF):
        nc, K = self.nc, self.K
        st = ExitStack()
        K.scope()
        bq, bk, bv, blf = self.dbufs[dQT.tensor.name], self.dbufs[dKT.tensor.name], self.dbufs[dV.tensor.name], self.dbufs[dLF.tensor.name]
        pfx = "A%d_" % L
        gqd, gqd_b = self.load_bcast(st, pfx + "gqd", P["diff_q_norm"], 64)
        gkd, gkd_b = self.load_bcast(st, pfx + "gkd", P["diff_k_norm"], 64)
        gqf, gqf_b = self.load_bcast(st, pfx + "gqf", P["fox_q_norm"], 128)
        gkf, gkf_b = self.load_bcast(st, pfx + "gkf", P["fox_k_norm"], 128)
        gqa, gqa_b = self.load_bcast(st, pfx + "gqa", P["mla_q_a_norm"], 512)
        gkva, gkva_b = self.load_bcast(st, pfx + "gkva", P["mla_kv_a_norm"], 256)
        gqm, gqm_b = self.load_bcast(st, pfx + "gqm", P["mla_q_norm"], 192)
        gkm, gkm_b = self.load_bcast(st, pfx + "gkm", P["mla_k_norm"], 192)
        fb, fb_b = self.load_bcast(st, pfx + "fb", P["fox_forget_bias"], 6)
        cgroup = [gqd_b, gkd_b, gqf_b, gkf_b, gqa_b, gkva_b, gqm_b, gkm_b, fb_b]
        _group_consts(K, "const", cgroup)
        K.op("dve", lambda e: e.tensor_scalar(out=gqd[:], in0=gqd[:], scalar1=64 ** -0.5, scalar2=None, op0=ALU.mult), writes=[gqd_b])
        K.op("dve", lambda e: e.tensor_scalar(out=gqf[:], in0=gqf[:], scalar1=128 ** -0.5, scalar2=None, op0=ALU.mult), writes=[gqf_b])
        K.op("dve", lambda e: e.tensor_scalar(out=gqm[:], in0=gqm[:], scalar1=192 ** -0.5, scalar2=None, op0=ALU.mult), writes=[gqm_b])
        cosP = self.sb(st, pfx + "cosP", [128, NB, 8], F32)
        sinP = self.sb(st, pfx + "sinP", [128, NB, 8], F32)
        cosM = self.sb(st, pfx + "cosM", [128, NB, 32], F32)
        sinM = self.sb(st, pfx + "sinM", [128, NB, 32], F32)
        tab_b = K.buf()
        for t, nm in ((cosP, "c_cosP"), (sinP, "c_sinP"), (cosM, "c_cosM"), (sinM, "c_sinM")):
            K.dma("sp", "tab", t[:], self.ins[nm].rearrange("(j i) d -> i j d", i=128), pwrites=[tab_b])
        wqu = self.sb(st, pfx + "wqu", [128, 4, 1152], BF16)
        wkvu = self.sb(st, pfx + "wkvu", [128, 2, 1536], BF16)
        wqu_b, wkvu_b = K.buf(), K.buf()
        self.wload("wsm", wqu[:], P["mla_w_q_up"].rearrange("(k p) c -> p k c", p=128), wqu_b)
        self.wload("wsm", wkvu[:], P["mla_w_kv_up"].rearrange("(k p) c -> p k c", p=128), wkvu_b)
        _group_consts(K, "wsm", [wqu_b, wkvu_b])

        hT = self.sb(st, pfx + "hT", [128, 16, NT], BF16)
        hT_b = K.buf()
        with ExitStack() as st2:
            K.scope()
            xs, xs_b = get_xs(st2)
            g_mix, g_mix_b = self.load_bcast(st2, pfx + "gmix", P["norm_mix"], D, key="gmix")
            self.rms_to_hT(xs, xs_b, g_mix, g_mix_b, hT, hT_b, st2)
            K.end_scope()

        wb = [self.sb(st, pfx + "wb%d" % i, [128, 16, 512], BF16) for i in range(2)]
        wb_b = [K.buf() for _ in range(2)]
        stg = [self.sb(st, pfx + "stg%d" % i, [128, 4, NT], BF16) for i in range(2)]
        stg_b = [K.buf() for _ in range(2)]
        stgm = self.sb(st, pfx + "stgm", [128, 9, NT], BF16)
        stgm_b = K.buf()
        vstg = self.sb(st, pfx + "vstg", [128, NB, 768], BF16)
        vstg_b = K.buf()
        lfs = self.sb(st, pfx + "lfs", [128, NB, 6], F32)
        lfs_b = K.buf()
        scr = self.sb(st, pfx + "scr", [128, 512], F32)
        scr_b = K.buf()
        tmp = self.sb(st, pfx + "tmp", [128, 512], F32)
        tmp_b = K.buf()
        qn = self.sb(st, pfx + "qn", [128, 512], F32)
        qn_b = K.buf()
        rtmp = self.sb(st, pfx + "rtmp", [128, 512], F32)
        rtmp_b = K.buf()
        rs = self.sb(st, pfx + "rs", [128, 16], F32)
        rs_b = K.buf()
        tm = [self.sb(st, pfx + "tm%d" % i, [128, 512], BF16) for i in range(2)]
        tm_b = [K.buf() for _ in range(2)]
        tmc = [0]

        def next_tm():
            tmc[0] += 1
            i = tmc[0] % 2
            return tm[i], tm_b[i]

        tiles = [(0, 512), (512, 512), (1024, 512), (1536, 512), (2048, 512), (2560, 512), (3072, 512),
                 (3584, 262), (3846, 512), (4358, 320)]
        w_in = P["w_in"]

        def load_w(ci):
            c0, w = tiles[ci]
            s = ci % 2
            self.wload("wb%d" % s, wb[s][:, :, 0:w], w_in[:, c0:c0 + w].rearrange("(k p) c -> p k c", p=128), wb_b[s])

        import os
        KSTOP = int(os.environ.get("KSTOP", "99"))
        if KSTOP <= 0:
            K.end_scope(); st.close(); return
        load_w(0)
        pbank = [0]

        def proj(tb, s, w):
            bi = pbank[0] % 3
            pbank[0] += 1
            ps = self.bank[bi]
            for kc in range(16):
                K.op("pe", lambda e, kc=kc: e.matmul(ps[:, 0:w], lhsT=hT[:, kc, tb * 128:(tb + 1) * 128], rhs=wb[s][:, kc, 0:w],
                                                     start=(kc == 0), stop=(kc == 15)),
                     reads=[hT_b, wb_b[s]], writes=[self.bbuf[bi]] if kc == 0 else (), pwrites=() if kc == 0 else [self.bbuf[bi]],
                     inc=(kc == 15))
            return ps, self.bbuf[bi]

        def qk_head_tile(ps, ps_b, tb, col0, nh, dh, gain, gain_b, rope_half, dst, dst_b, dst_t0):
            src3 = ps[:, col0:col0 + nh * dh].rearrange("p (h d) -> p h d", h=nh)
            self.rstd_of(src3, ps_b, nh, dh, scr, scr_b, rs, rs_b)
            t, t_b = next_tm()
            t3 = t[:, 0:nh * dh].rearrange("p (h d) -> p h d", h=nh)
            if rope_half:
                q3 = qn[:, 0:nh * dh].rearrange("p (h d) -> p h d", h=nh)
                self.norm_gain(q3, qn_b, src3, ps_b, rs, rs_b, gain, gain_b, nh, dh, tmp, tmp_b)
                K.op("act", lambda e: e.copy(out=t3[:, :, 2 * rope_half:], in_=q3[:, :, 2 * rope_half:]), reads=[qn_b], writes=[t_b])
                self.rope(t3, t_b, q3, qn_b, cosP[:, tb, :], sinP[:, tb, :], tab_b, nh, rope_half, rtmp, rtmp_b)
            else:
                self.norm_gain(t3, t_b, src3, ps_b, rs, rs_b, gain, gain_b, nh, dh, tmp, tmp_b)
            nch = nh * dh // 128
            self.transpose_to(t, t_b, nch, dst[:, dst_t0:dst_t0 + nch, tb * 128:(tb + 1) * 128], dst_b, bank_i=6,
                              evac="act")

        def v_copy(ps, ps_b, tb, col0, ncol, vcol0):
            K.op("act", lambda e: e.copy(out=vstg[:, tb, vcol0:vcol0 + ncol], in_=ps[:, col0:col0 + ncol]),
                 reads=[ps_b], pwrites=[vstg_b])

        def flush_stg(s, n, kind, t0):
            d, db = (dQT, bq) if kind == "Q" else (dKT, bk)
            for i in range(n):
                K.dma("sp", "stg%d" % s, d[(t0 + i) * 128:(t0 + i + 1) * 128, :], stg[s][:, i, :], reads=[stg_b[s]], pwrites=[db])

        def flush_v(nheads, h0):
            for h in range(nheads):
                K.dma("sp", "vstg", dV[(h0 + h) * NT:(h0 + h + 1) * NT, :].rearrange("(j i) d -> i j d", i=128),
                      vstg[:, :, h * 128:(h + 1) * 128], reads=[vstg_b], pwrites=[bv])

        for ci in range(len(tiles)):
            if ci >= KSTOP:
                break
            c0, w = tiles[ci]
            s = ci % 2
            if ci + 1 < len(tiles):
                load_w(ci + 1)
            sg = ci % 2
            if ci in (0, 1, 3, 4, 5):
                K.wait_all("act", [stg_b[sg]])
                K.wait_all("dve", [stg_b[sg]])
            for tb in range(NB):
                ps, ps_b = proj(tb, s, w)
                if ci == 0:
                    qk_head_tile(ps, ps_b, tb, 0, 8, 64, gqd[:], gqd_b, 8, stg[sg], stg_b[sg], 0)
                elif ci == 1:
                    qk_head_tile(ps, ps_b, tb, 0, 8, 64, gkd[:], gkd_b, 8, stg[sg], stg_b[sg], 0)
                elif ci == 2:
                    v_copy(ps, ps_b, tb, 0, 512, 0)
                elif ci == 3:
                    qk_head_tile(ps, ps_b, tb, 0, 4, 128, gqf[:], gqf_b, 0, stg[sg], stg_b[sg], 0)
                elif ci == 4:
                    qk_head_tile(ps, ps_b, tb, 0, 2, 128, gqf[:], gqf_b, 0, stg[sg], stg_b[sg], 0)
                    qk_head_tile(ps, ps_b, tb, 256, 2, 128, gkf[:], gkf_b, 0, stg[sg], stg_b[sg], 2)
                elif ci == 5:
                    qk_head_tile(ps, ps_b, tb, 0, 4, 128, gkf[:], gkf_b, 0, stg[sg], stg_b[sg], 0)
                elif ci == 6:
                    v_copy(ps, ps_b, tb, 0, 512, 0)
                elif ci == 7:
                    v_copy(ps, ps_b, tb, 0, 256, 0)
                    if int(os.environ.get("KVAR", "0")) == 1:
                        continue
                    kv = int(os.environ.get("KVAR", "0"))
                    if kv != 6:
                        K.op("act", lambda e: e.copy(out=rs[:, 8:14], in_=ps[:, 256:262]), reads=[ps_b], writes=[rs_b])
                        K.op("dve", lambda e: e.tensor_tensor(out=rs[:, 8:14], in0=rs[:, 8:14], in1=fb[:], op=ALU.add),
                             reads=[fb_b], writes=[rs_b])
                    if kv in (5, 6):
                        kv = 4
                    if int(os.environ.get("KVAR", "0")) == 5:
                        continue
                    if kv != 4:
                        K.op("act", lambda e: e.activation(out=rs[:, 8:14], in_=rs[:, 8:14], func=AF.Exp, scale=-1.0), writes=[rs_b])
                    if kv not in (3, 4):
                        K.op("act", lambda e: e.activation(out=rs[:, 8:14], in_=rs[:, 8:14], func=AF.Ln, bias=self.oneT[:, 0:1]), reads=[self.b_eps], writes=[rs_b])
                    K.op("dve", lambda e: e.tensor_scalar(out=lfs[:, tb, :], in0=rs[:, 8:14], scalar1=-1.0, scalar2=None, op0=ALU.mult),
                         reads=[rs_b], pwrites=[lfs_b])
                elif ci == 8:
                    self.mla_q(L, ps, ps_b, tb, locals())
                elif ci == 9:
                    self.mla_kv(L, ps, ps_b, tb, locals())
            if ci == 0:
                flush_stg(sg, 4, "Q", 0)
            elif ci == 1:
                flush_stg(sg, 4, "K", 0)
            elif ci == 2:
                flush_v(4, 0)
            elif ci == 3:
                flush_stg(sg, 4, "Q", 4)
            elif ci == 4:
                for i in range(2):
                    K.dma("sp", "stg%d" % sg, dQT[(8 + i) * 128:(9 + i) * 128, :], stg[sg][:, i, :], reads=[stg_b[sg]], pwrites=[bq])
                for i in range(2):
                    K.dma("sp", "stg%d" % sg, dKT[(4 + i) * 128:(5 + i) * 128, :], stg[sg][:, 2 + i, :], reads=[stg_b[sg]], pwrites=[bk])
            elif ci == 5:
                flush_stg(sg, 4, "K", 6)
            elif ci == 6:
                flush_v(4, 4)
            elif ci == 7:
                flush_v(2, 8)
                if int(os.environ.get("KVAR", "0")) not in (1, 2):
                    K.dma("sp", "lfs", dLF.rearrange("(j i) h -> i j h", i=128), lfs[:], reads=[lfs_b], pwrites=[blf])
            elif ci == 8:
                for i in range(9):
                    K.dma("sp", "stgm", dQT[(10 + i) * 128:(11 + i) * 128, :], stgm[:, i, :], reads=[stgm_b], pwrites=[bq])
            elif ci == 9:
                for i in range(7):
                    K.dma("sp", "stgm", dKT[(10 + i) * 128:(11 + i) * 128, :], stgm[:, i, :], reads=[stgm_b], pwrites=[bk])
                flush_v(6, 10)
        K.wait_all("sp", [bq, bk, bv, blf, stg_b[0], stg_b[1], stgm_b, vstg_b, lfs_b])
        K.end_scope()
        st.close()

    def mla_q(self, L, ps, ps_b, tb, env):
        K = self.K
        scr, scr_b, rs, rs_b, tmp, tmp_b = env["scr"], env["scr_b"], env["rs"], env["rs_b"], env["tmp"], env["tmp_b"]
        qn, qn_b, rtmp, rtmp_b = env["qn"], env["qn_b"], env["rtmp"], env["rtmp_b"]
        gqa, gqa_b, gqm, gqm_b = env["gqa"], env["gqa_b"], env["gqm"], env["gqm_b"]
        wqu, wqu_b, stgm, stgm_b = env["wqu"], env["wqu_b"], env["stgm"], env["stgm_b"]
        cosM, sinM, tab_b = env["cosM"], env["sinM"], env["tab_b"]
        next_tm = env["next_tm"]
        st = env["st"]
        pfx = "A%d_" % L
        if not hasattr(self, "_mqT_%d" % L):
            setattr(self, "_mqT_%d" % L, (self.sb(st, pfx + "mqT", [128, 4, 128], BF16), K.buf()))
        mqT, mqT_b = getattr(self, "_mqT_%d" % L)
        if tb == 0:
            K.wait_all("act", [stgm_b])
            K.wait_all("dve", [stgm_b])
        src3 = ps[:, 0:512].rearrange("p (h d) -> p h d", h=1)
        self.rstd_of(src3, ps_b, 1, 512, scr, scr_b, rs, rs_b)
        t, t_b = next_tm()
        self.norm_gain(t[:, 0:512].rearrange("p (h d) -> p h d", h=1), t_b, src3, ps_b, rs, rs_b, gqa[:], gqa_b, 1, 512, tmp, tmp_b)
        K.wait_all("act", [mqT_b])
        self.transpose_to(t, t_b, 4, mqT[:, :, :], mqT_b, bank_i=6, evac="act")
        for m in range(3):
            bi = 3 + (m % 2)
            p2 = self.bank[bi]
            p2_b = self.bbuf[bi]
            for kc in range(4):
                K.op("pe", lambda e, kc=kc: e.matmul(p2[:, 0:384], lhsT=mqT[:, kc, :], rhs=wqu[:, kc, m * 384:(m + 1) * 384],
                                                     start=(kc == 0), stop=(kc == 3)),
                     reads=[mqT_b, wqu_b], writes=[p2_b] if kc == 0 else (), pwrites=() if kc == 0 else [p2_b], inc=(kc == 3))
            v3 = p2[:, 0:384].rearrange("p (h d) -> p h d", h=2)
            rope3 = v3[:, :, 0:64]
            nope3 = v3[:, :, 64:192]
            self.rstd_of(nope3, p2_b, 2, 128, scr, scr_b, rs, rs_b)
            t, t_b = next_tm()
            self.norm_gain(t[:, 0:256].rearrange("p (h d) -> p h d", h=2), t_b, nope3, p2_b, rs, rs_b, gqm[:, 64:192], gqm_b,
                           2, 128, tmp, tmp_b)
            self.transpose_to(t, t_b, 2, stgm[:, 2 * m:2 * m + 2, tb * 128:(tb + 1) * 128], stgm_b, bank_i=6, evac="act")
            self.rstd_of(rope3, p2_b, 2, 64, scr, scr_b, rs, rs_b)
            q3 = qn[:, 0:128].rearrange("p (h d) -> p h d", h=2)
            self.norm_gain(q3, qn_b, rope3, p2_b, rs, rs_b, gqm[:, 0:64], gqm_b, 2, 64, tmp, tmp_b)
            t, t_b = next_tm()
            t3 = t[:, 0:128].rearrange("p (h d) -> p h d", h=2)
            K.wait_all("dve", [t_b])
            self.rope(t3, t_b, q3, qn_b, cosM[:, tb, :], sinM[:, tb, :], tab_b, 2, 32, rtmp, rtmp_b)
            self.transpose_to(t, t_b, 1, stgm[:, 6 + m:7 + m, tb * 128:(tb + 1) * 128], stgm_b, bank_i=7, evac="dve")

    def mla_kv(self, L, ps, ps_b, tb, env):
        K = self.K
        scr, scr_b, rs, rs_b, tmp, tmp_b = env["scr"], env["scr_b"], env["rs"], env["rs_b"], env["tmp"], env["tmp_b"]
        qn, qn_b, rtmp, rtmp_b = env["qn"], env["qn_b"], env["rtmp"], env["rtmp_b"]
        gkva, gkva_b, gkm, gkm_b = env["gkva"], env["gkva_b"], env["gkm"], env["gkm_b"]
        wkvu, wkvu_b, stgm, stgm_b = env["wkvu"], env["wkvu_b"], env["stgm"], env["stgm_b"]
        vstg, vstg_b = env["vstg"], env["vstg_b"]
        cosM, sinM, tab_b = env["cosM"], env["sinM"], env["tab_b"]
        next_tm = env["next_tm"]
        st = env["st"]
        pfx = "A%d_" % L
        if not hasattr(self, "_ckT_%d" % L):
            setattr(self, "_ckT_%d" % L, (self.sb(st, pfx + "ckT", [128, 2, 128], BF16), K.buf()))
        ckT, ckT_b = getattr(self, "_ckT_%d" % L)
        if tb == 0:
            K.wait_all("act", [stgm_b, vstg_b])
            K.wait_all("dve", [stgm_b, vstg_b])
        src3 = ps[:, 0:256].rearrange("p (h d) -> p h d", h=1)
        self.rstd_of(src3, ps_b, 1, 256, scr, scr_b, rs, rs_b)
        t, t_b = next_tm()
        self.norm_gain(t[:, 0:256].rearrange("p (h d) -> p h d", h=1), t_b, src3, ps_b, rs, rs_b, gkva[:], gkva_b, 1, 256, tmp, tmp_b)
        K.wait_all("act", [ckT_b])
        self.transpose_to(t, t_b, 2, ckT[:, :, :], ckT_b, bank_i=6, evac="act")
        kr3 = ps[:, 256:320].rearrange("p (h d) -> p h d", h=1)
        self.rstd_of(kr3, ps_b, 1, 64, scr, scr_b, rs, rs_b)
        q3 = qn[:, 0:64].rearrange("p (h d) -> p h d", h=1)
        self.norm_gain(q3, qn_b, kr3, ps_b, rs, rs_b, gkm[:, 0:64], gkm_b, 1, 64, tmp, tmp_b)
        t, t_b = next_tm()
        K.wait_all("dve", [t_b])
        self.rope(t[:, 0:64].rearrange("p (h d) -> p h d", h=1), t_b, q3, qn_b, cosM[:, tb, :], sinM[:, tb, :], tab_b, 1, 32,
                  rtmp, rtmp_b)
        K.op("dve", lambda e: e.tensor_copy(out=t[:, 64:128], in_=t[:, 0:64]), reads=[t_b], pwrites=[t_b])
        self.transpose_to(t, t_b, 1, stgm[:, 6:7, tb * 128:(tb + 1) * 128], stgm_b, bank_i=7, evac="dve")
        for m in range(3):
            bi = 3 + (m % 2)
            p2 = self.bank[bi]
            p2_b = self.bbuf[bi]
            for kc in range(2):
                K.op("pe", lambda e, kc=kc: e.matmul(p2[:, 0:512], lhsT=ckT[:, kc, :], rhs=wkvu[:, kc, m * 512:(m + 1) * 512],
                                                     start=(kc == 0), stop=(kc == 1)),
                     reads=[ckT_b, wkvu_b], writes=[p2_b] if kc == 0 else (), pwrites=() if kc == 0 else [p2_b], inc=(kc == 1))
            v3 = p2[:, 0:512].rearrange("p (h d) -> p h d", h=2)
            kn3 = v3[:, :, 0:128]
            vv3 = v3[:, :, 128:256]
            self.rstd_of(kn3, p2_b, 2, 128, scr, scr_b, rs, rs_b)
            t, t_b = next_tm()
            self.norm_gain(t[:, 0:256].rearrange("p (h d) -> p h d", h=2), t_b, kn3, p2_b, rs, rs_b, gkm[:, 64:192], gkm_b,
                           2, 128, tmp, tmp_b)
            self.transpose_to(t, t_b, 2, stgm[:, 2 * m:2 * m + 2, tb * 128:(tb + 1) * 128], stgm_b, bank_i=6, evac="act")
            K.op("act", lambda e: e.copy(out=vstg[:, tb, m * 256:(m + 1) * 256].rearrange("p (h d) -> p h d", h=2), in_=vv3),
                 reads=[p2_b], pwrites=[vstg_b])


def _group_consts(K, key, bufs):
    tot = K.dsem[key][1]
    for b in bufs:
        b.w = {key: tot}


def phase_B(self, L, P, dQT, dKTg, dVg, dLFg, dCR, get_xs, lam_ap):
    nc, K = self.nc, self.K
    pfx = "B%d_" % L
    bq, bk, bv, blf, bcr = (self.dbufs[d.tensor.name] for d in (dQT, dKTg, dVg, dLFg, dCR))
    st = ExitStack()
    sa = ExitStack()
    mixT = self.sb(st, pfx + "mixT", [128, 16, NT], BF16); mixT_b = Buf("mixT")
    K.scope()
    cb = []
    mask = self.sb(sa, pfx + "mask", [128, 8, 128], BF16); mask_b = K.buf(); cb.append(mask_b)
    K.dma("sp", "const", mask[:], self.ins["c_mask"], writes=[mask_b])
    onehot = self.sb(sa, pfx + "onehot", [128, 8], F32); oh_b = K.buf(); cb.append(oh_b)
    K.dma("sp", "const", onehot[:], self.ins["c_onehot"], writes=[oh_b])
    tri = self.sb(sa, pfx + "tri", [128, 128], F32); tri_b = K.buf(); cb.append(tri_b)
    K.dma("sp", "const", tri[:], self.ins["c_tri"], writes=[tri_b])
    sl = self.sb(sa, pfx + "sl", [64, 64], F32); sl_b = K.buf(); cb.append(sl_b)
    K.dma("sp", "const", sl[:], self.ins["c_sl"], writes=[sl_b])
    lq = self.sb(sa, pfx + "lq", [128, 4, 64], F32); lq_b = K.buf(); cb.append(lq_b)
    for i, nm in enumerate(("diff_lambda_q1", "diff_lambda_k1", "diff_lambda_q2", "diff_lambda_k2")):
        K.dma("sp", "const", lq[:, i, :], P[nm].partition_broadcast(128), pwrites=[lq_b])
    gsub, gsub_b = self.load_bcast(sa, pfx + "gsub", P["diff_subln"], 128); cb.append(gsub_b)
    lami, lami_b = self.load_bcast(sa, pfx + "lami", lam_ap, 1); cb.append(lami_b)
    _group_consts(K, "const", cb)
    ones = self.sb(sa, pfx + "ones", [128, 128], F32); ones_b = K.buf()
    K.op("dve", lambda e: e.memset(ones[:], 1.0), writes=[ones_b])
    onesb = self.sb(sa, pfx + "onesb", [128, 128], BF16); onesb_b = K.buf()
    K.op("dve", lambda e: e.memset(onesb[:], 1.0), writes=[onesb_b])
    sm = self.sb(sa, pfx + "sm", [128, 16], F32); sm_b = K.buf()
    import os
    KP = int(os.environ.get("KP", "9"))
    neglam = self.sb(sa, pfx + "neglam", [128, 1], F32); neglam_b = K.buf()
    if KP >= 1:
        lt = self.sb(sa, pfx + "lt", [128, 2, 64], F32); lt_b = K.buf()
        K.op("dve", lambda e: e.tensor_tensor(out=lt[:, 0, :], in0=lq[:, 0, :], in1=lq[:, 1, :], op=ALU.mult), reads=[lq_b], writes=[lt_b])
        K.op("dve", lambda e: e.tensor_tensor(out=lt[:, 1, :], in0=lq[:, 2, :], in1=lq[:, 3, :], op=ALU.mult), reads=[lq_b], pwrites=[lt_b])
        K.op("dve", lambda e: e.reduce_sum(out=sm[:, 0:2], in_=lt[:], axis=AX.X), reads=[lt_b], writes=[sm_b])
        K.op("act", lambda e: e.activation(out=sm[:, 0:2], in_=sm[:, 0:2], func=AF.Exp), writes=[sm_b])
        K.op("dve", lambda e: e.tensor_tensor(out=sm[:, 2:3], in0=sm[:, 1:2], in1=sm[:, 0:1], op=ALU.subtract), writes=[sm_b])
        K.op("dve", lambda e: e.tensor_tensor(out=neglam[:], in0=sm[:, 2:3], in1=lami[:], op=ALU.subtract), reads=[sm_b, lami_b], writes=[neglam_b])
        K.op("dve", lambda e: e.tensor_scalar(out=sm[:, 3:4], in0=lami[:], scalar1=-1.0, scalar2=1.0, op0=ALU.mult, op1=ALU.add),
             reads=[lami_b], writes=[sm_b])
        K.op("dve", lambda e: e.tensor_scalar(out=gsub[:], in0=gsub[:], scalar1=sm[:, 3:4], scalar2=None, op0=ALU.mult),
             reads=[sm_b], writes=[gsub_b])

    KB = int(os.environ.get("KB", "99"))
    negc = self.sb(sa, pfx + "negc", [128, 64, 6], F32); negc_b = K.buf()
    cum = self.sb(sa, pfx + "cum", [128, 64, 6], F32); cum_b = K.buf()
    if KP >= 2:
        lf = self.sb(sa, pfx + "lf", [128, 64, 6], F32); lf_b = K.buf()
        for j in range(8):
            K.dma("sp", "lf", lf[:, j * 8:(j + 1) * 8, :],
                  dLFg.rearrange("(r j i) h -> j i r h", r=8, j=8)[j], reads=[blf], pwrites=[lf_b])
        cps = self.bank[0]; cps_b = self.bbuf[0]
        tps = self.bank[1]; tps_b = self.bbuf[1]
        K.op("pe", lambda e: e.matmul(cps[:, 0:384], lhsT=tri[:], rhs=lf[:].rearrange("p g h -> p (g h)"), start=True, stop=False),
             reads=[tri_b, lf_b], writes=[cps_b])
        for h in range(6):
            K.op("pe", lambda e, h=h: e.matmul(tps[0:64, h:h + 1], lhsT=lf[:, :, h], rhs=ones[:, 0:1], start=True, stop=True),
                 reads=[lf_b, ones_b], writes=[tps_b] if h == 0 else (), pwrites=() if h == 0 else [tps_b])
        tot = self.sb(sa, pfx + "tot", [64, 6], F32); tot_b = K.buf()
        K.op("act", lambda e: e.copy(out=tot[:], in_=tps[0:64, 0:6]), reads=[tps_b], writes=[tot_b])
        R = self.sb(sa, pfx + "R", [64, 64, 6], F32); R_b = K.buf()
        K.op("dve", lambda e: e.tensor_tensor(out=R[:], in0=tot[:].unsqueeze(1).broadcast_to([64, 64, 6]),
                                              in1=sl[:].unsqueeze(2).broadcast_to([64, 64, 6]), op=ALU.mult),
             reads=[sl_b, tot_b], writes=[R_b])
        K.op("pe", lambda e: e.matmul(cps[:, 0:384], lhsT=ones[0:64, :], rhs=R[:].rearrange("p g h -> p (g h)"), start=False, stop=True),
             reads=[ones_b, R_b], pwrites=[cps_b])
        K.op("dve", lambda e: e.tensor_scalar(out=negc[:].rearrange("p g h -> p (g h)"), in0=cps[:, 0:384], scalar1=-1.0, scalar2=None,
                                              op0=ALU.mult), reads=[cps_b], writes=[negc_b])
        K.op("dve", lambda e: e.tensor_copy(out=cum[:].rearrange("p g h -> p (g h)"), in_=cps[:, 0:384]), reads=[cps_b], writes=[cum_b])
    crow = []
    crow_b = K.buf()
    if KP >= 3:
        t4 = self.sb(sa, pfx + "t4", [128, 8, 8, 6], F32); t4_b = K.buf()
        K.op("dve", lambda e: e.tensor_tensor(out=t4[:], in0=cum[:].rearrange("p (j r) h -> p j r h", j=8),
                                              in1=onehot[:].unsqueeze(1).unsqueeze(3).broadcast_to([128, 8, 8, 6]), op=ALU.mult),
             reads=[cum_b, oh_b], writes=[t4_b])
        own = self.sb(sa, pfx + "own", [128, 8, 6], F32); own_b = K.buf()
        K.op("dve", lambda e: e.reduce_sum(out=own[:], in_=t4[:].rearrange("p j r h -> p j h r"), axis=AX.X), reads=[t4_b], writes=[own_b])
        parts = self.sb(sa, pfx + "parts", [128, 6, 3, 8], BF16); parts_b = K.buf()
        r1 = self.sb(sa, pfx + "r1", [128, 6, 8], F32); r1_b = K.buf()
        own_hj = own[:].rearrange("p j h -> p h j")
        K.op("dve", lambda e: e.tensor_copy(out=parts[:, :, 0, :], in_=own_hj), reads=[own_b], writes=[parts_b])
        K.op("dve", lambda e: e.tensor_tensor(out=r1[:], in0=own_hj, in1=parts[:, :, 0, :], op=ALU.subtract), reads=[own_b, parts_b], writes=[r1_b])
        K.op("dve", lambda e: e.tensor_copy(out=parts[:, :, 1, :], in_=r1[:]), reads=[r1_b], pwrites=[parts_b])
        K.op("dve", lambda e: e.tensor_tensor(out=r1[:], in0=r1[:], in1=parts[:, :, 1, :], op=ALU.subtract), reads=[parts_b], writes=[r1_b])
        K.op("dve", lambda e: e.tensor_copy(out=parts[:, :, 2, :], in_=r1[:]), reads=[r1_b], pwrites=[parts_b])
        pf = parts[:].rearrange("p h k j -> p (h k j)")
        pb7 = self.bank[7][:, :].bitcast(BF16); b7 = self.bbuf[7]
        K.op("pe", lambda e: e.transpose(out=pb7[:, 0:128], in_=pf[:, 0:128], identity=self.ident[:]), reads=[parts_b, self.b_ident], writes=[b7])
        K.op("pe", lambda e: e.transpose(out=pb7[0:16, 128:256], in_=pf[:, 128:144], identity=self.ident[:]), reads=[parts_b, self.b_ident], pwrites=[b7])
        crT = self.sb(sa, pfx + "crT", [128, 2, 128], BF16); crT_b = K.buf()
        K.op("dve", lambda e: e.tensor_copy(out=crT[:, 0, :], in_=pb7[:, 0:128]), reads=[b7], writes=[crT_b])
        K.op("dve", lambda e: e.tensor_copy(out=crT[0:16, 1, :], in_=pb7[0:16, 128:256]), reads=[b7], pwrites=[crT_b])
        K.dma("sp", "cr", dCR[0:128, :], crT[:, 0, :], reads=[crT_b], writes=[bcr])
        K.dma("sp", "cr", dCR[128:144, :], crT[0:16, 1, :], reads=[crT_b], pwrites=[bcr])
        for h in range(6):
            t = self.sb(sa, pfx + "crow%d" % h, [3, NT], BF16)
            K.dma("sp", "crow", t[:], dCR[h * 24:(h + 1) * 24, :].rearrange("(k j) i -> k (j i)", k=3), reads=[bcr], pwrites=[crow_b])
            crow.append(t)

    kt = [self.sb(sa, pfx + "kt%d" % i, [128, S], BF16) for i in range(2)]
    kt_b = [K.buf() for _ in range(2)]
    vt = [self.sb(sa, pfx + "vt%d" % i, [128, 64, 132], BF16) for i in range(2)]
    vt_b = [K.buf() for _ in range(2)]
    qt = [self.sb(sa, pfx + "qt%d" % i, [128, 2, NT], BF16) for i in range(2)]
    qt_b = [K.buf() for _ in range(2)]
    kr = self.sb(sa, pfx + "kr", [128, S], BF16); kr_b = K.buf()
    pt = [self.sb(sa, pfx + "pt%d" % i, [128, 512], BF16) for i in range(3)]
    pt_b = [K.buf() for _ in range(3)]
    otm = [self.sb(sa, pfx + "otm%d" % i, [128, 512], BF16) for i in range(2)]
    otm_b = [K.buf() for _ in range(2)]
    d1 = self.sb(sa, pfx + "d1", [128, 128], F32); d1_b = K.buf()
    dd = self.sb(sa, pfx + "dd", [128, 128], F32); dd_b = K.buf()
    dsq = self.sb(sa, pfx + "dsq", [128, 128], F32); dsq_b = K.buf()
    rinv = self.sb(sa, pfx + "rinv", [128, 8], F32); rinv_b = K.buf()
    for i in range(2):
        K.op("pool", lambda e, i=i: e.memset(vt[i][:, :, 128:132], 1.0), writes=[vt_b[i]])
    kg = dKTg.rearrange("(r t p) c -> t p r c", r=8, t=NKT)
    vg = dVg.rearrange("(r h j i) d -> h r i j d", r=8, h=NH, j=8)

    def load_head(hd, slot):
        kti = hd
        K.dma("sp", "kt%d" % slot, kt[slot][:].rearrange("p (r c) -> p r c", r=8), kg[kti], reads=[bk], writes=[kt_b[slot]])
        for r in range(8):
            K.dma("sp", "vt%d" % slot, vt[slot][:, r * 8:(r + 1) * 8, 0:128], vg[hd, r], reads=[bv], pwrites=[vt_b[slot]])
        K.dma("sp", "qt%d" % slot, qt[slot][:, 0, :], dQT[hd * 128:(hd + 1) * 128, :], reads=[bq], writes=[qt_b[slot]])
        if hd >= 10:
            m = hd - 10
            K.dma("sp", "qt%d" % slot, qt[slot][:, 1, :], dQT[(16 + m // 2) * 128:(17 + m // 2) * 128, :], reads=[bq], pwrites=[qt_b[slot]])

    def claim_vt(slot):
        K.wait_all("sp", [vt_b[slot]])

    K.dma("sp", "kr", kr[:].rearrange("p (r c) -> p r c", r=8), kg[16], reads=[bk], writes=[kr_b])
    if KB > 0:
        claim_vt(0)
        load_head(0, 0)

    sctr = [0]
    accsets = [(3, 4), (5, 6)]

    def accb(si, jj):
        return self.bbuf[accsets[si][jj // 2]]

    def acc_ap(si, jj):
        return self.bank[accsets[si][jj // 2]][:, (jj % 2) * 256:(jj % 2) * 256 + 130]

    def attn_tile(hd, slot, T, mp, si):
        kind = "diff" if hd < 4 else ("fox" if hd < 10 else "mla")
        if kind == "diff":
            plo, phi = 64 * mp, 64 * mp + 64
        else:
            plo, phi = 0, 128
        steps = []
        for jp in range(4 * T):
            for r in range(8):
                steps.append((jp, r, 0, None))
        for a in range(4):
            for r in range(8):
                steps.append((4 * T + a, r, a * 128, r))
        first = True
        for (jp, r, c0, mr) in steps:
            kb = r * 8 + jp
            bi = sctr[0] % 3
            sctr[0] += 1
            sp_, sb_ = self.bank[bi], self.bbuf[bi]
            qcols = slice(T * 512 + c0, T * 512 + 512)
            last_extra = (kind != "fox") and (mr is None) and (kind != "mla")
            nmm = 1 + (1 if kind == "mla" else 0) + (1 if kind == "fox" else 0) + (1 if mr is not None else 0)
            idx = [0]

            def fl():
                i = idx[0]
                idx[0] += 1
                return dict(start=(i == 0), stop=(i == nmm - 1)), (i == nmm - 1), (i == 0)
            f, inc, w0 = fl()
            K.op("pe", lambda e: e.matmul(sp_[:, c0:512], lhsT=kt[slot][plo:phi, kb * 128:(kb + 1) * 128],
                                          rhs=qt[slot][plo:phi, 0, qcols], **f),
                 reads=[kt_b[slot], qt_b[slot]], writes=[sb_], inc=inc)
            if kind == "mla":
                m = hd - 10
                rlo = 64 * (m % 2)
                f, inc, _ = fl()
                K.op("pe", lambda e: e.matmul(sp_[:, c0:512], lhsT=kr[rlo:rlo + 64, kb * 128:(kb + 1) * 128],
                                              rhs=qt[slot][rlo:rlo + 64, 1, qcols], **f),
                     reads=[kr_b, qt_b[slot]], pwrites=[sb_], inc=inc)
            if kind == "fox":
                f, inc, _ = fl()
                K.op("pe", lambda e: e.matmul(sp_[:, c0:512], lhsT=onesb[0:3, :], rhs=crow[hd - 4][:, qcols], **f),
                     reads=[onesb_b, crow_b], pwrites=[sb_], inc=inc)
            if mr is not None:
                f, inc, _ = fl()
                K.op("pe", lambda e: e.matmul(sp_[:, c0:c0 + 128], lhsT=self.ident[:], rhs=mask[:, mr, :], **f),
                     reads=[self.b_ident, mask_b], pwrites=[sb_], inc=inc)
            pi = bi
            if kind == "fox":
                g = 8 * jp + r
                K.op("act", lambda e: e.activation(out=pt[pi][:, c0:512], in_=sp_[:, c0:512], func=AF.Exp,
                                                   bias=negc[:, g, hd - 4:hd - 3]),
                     reads=[sb_, negc_b], writes=[pt_b[pi]])
            else:
                K.op("act", lambda e: e.activation(out=pt[pi][:, c0:512], in_=sp_[:, c0:512], func=AF.Exp),
                     reads=[sb_], writes=[pt_b[pi]])
            jjs = list(range(c0 // 128, 4))
            for n, jj in enumerate(jjs):
                stp = (jp == 4 * T + jj and r == 7)
                K.op("pe", lambda e, jj=jj, stp=stp: e.matmul(acc_ap(si, jj), lhsT=pt[pi][:, jj * 128:(jj + 1) * 128],
                                                              rhs=vt[slot][:, kb, 0:130], start=(first and jj % 2 == 0), stop=stp,
                                                              skip_group_check=True),
                     reads=[pt_b[pi], vt_b[slot]], writes=[accb(si, jj)] if (first and jj % 2 == 0) else (),
                     pwrites=() if (first and jj % 2 == 0) else [accb(si, jj)], inc=(n == len(jjs) - 1))
            first = False

    octr = [0]

    def post_plain(hd, T, si):
        o = octr[0] % 2
        octr[0] += 1
        for jj in range(4):
            O = acc_ap(si, jj)
            K.op("dve", lambda e: e.reciprocal(out=rinv[:, jj:jj + 1], in_=O[:, 128:129]), reads=[accb(si, jj)],
                 writes=[rinv_b] if jj == 0 else (), pwrites=() if jj == 0 else [rinv_b])
            K.op("dve", lambda e: e.tensor_scalar(out=otm[o][:, jj * 128:(jj + 1) * 128], in0=O[:, 0:128], scalar1=rinv[:, jj:jj + 1],
                                                  scalar2=None, op0=ALU.mult),
                 reads=[accb(si, jj), rinv_b], writes=[otm_b[o]] if jj == 0 else (), pwrites=() if jj == 0 else [otm_b[o]])
        self.transpose_to(otm[o], otm_b[o], 4, mixT[:, hd, T * 512:(T + 1) * 512].rearrange("p (q t) -> p q t", q=4), mixT_b,
                          bank_i=7, evac="dve")

    def post_diff(hd, T):
        o = octr[0] % 2
        octr[0] += 1
        for jj in range(4):
            O1 = acc_ap(0, jj)
            O2 = acc_ap(1, jj)
            K.op("dve", lambda e: e.reciprocal(out=rinv[:, 0:1], in_=O1[:, 128:129]), reads=[accb(0, jj)], writes=[rinv_b])
            K.op("dve", lambda e: e.reciprocal(out=rinv[:, 1:2], in_=O2[:, 128:129]), reads=[accb(1, jj)], pwrites=[rinv_b])
            K.op("dve", lambda e: e.tensor_tensor(out=rinv[:, 2:3], in0=rinv[:, 1:2], in1=neglam[:], op=ALU.mult),
                 reads=[neglam_b], writes=[rinv_b])
            K.op("dve", lambda e: e.tensor_scalar(out=d1[:], in0=O1[:, 0:128], scalar1=rinv[:, 0:1], scalar2=None, op0=ALU.mult),
                 reads=[accb(0, jj), rinv_b], writes=[d1_b])
            K.op("dve", lambda e: e.scalar_tensor_tensor(out=dd[:], in0=O2[:, 0:128], scalar=rinv[:, 2:3], in1=d1[:],
                                                         op0=ALU.mult, op1=ALU.add),
                 reads=[accb(1, jj), rinv_b, d1_b], writes=[dd_b])
            K.op("act", lambda e: e.activation(out=dsq[:], in_=dd[:], func=AF.Square, accum_out=rinv[:, 3:4]),
                 reads=[dd_b], writes=[dsq_b, rinv_b])
            K.op("act", lambda e: e.activation(out=rinv[:, 3:4], in_=rinv[:, 3:4], func=AF.Sqrt, bias=self.epsT[:, 0:1],
                                               scale=1.0 / 128), reads=[self.b_eps], writes=[rinv_b])
            K.op("dve", lambda e: e.reciprocal(out=rinv[:, 3:4], in_=rinv[:, 3:4]), writes=[rinv_b])
            K.op("dve", lambda e: e.scalar_tensor_tensor(out=otm[o][:, jj * 128:(jj + 1) * 128], in0=dd[:], scalar=rinv[:, 3:4],
                                                         in1=gsub[:], op0=ALU.mult, op1=ALU.mult),
                 reads=[dd_b, rinv_b, gsub_b], writes=[otm_b[o]] if jj == 0 else (), pwrites=() if jj == 0 else [otm_b[o]])
        self.transpose_to(otm[o], otm_b[o], 4, mixT[:, hd, T * 512:(T + 1) * 512].rearrange("p (q t) -> p q t", q=4), mixT_b,
                          bank_i=7, evac="dve")

    sictr = [0]
    for hd in range(NH):
        if hd >= KB:
            break
        slot = hd % 2
        if hd + 1 < min(NH, KB):
            claim_vt(1 - slot)
            load_head(hd + 1, 1 - slot)
        for T in range(2):
            if hd < 4:
                attn_tile(hd, slot, T, 0, 0)
                attn_tile(hd, slot, T, 1, 1)
                post_diff(hd, T)
            else:
                si = sictr[0] % 2
                sictr[0] += 1
                attn_tile(hd, slot, T, 0, si)
                post_plain(hd, T, si)

    K.end_scope()
    sa.close()

    K.scope()
    xs, xs_b = get_xs(st)
    wb = [self.sb(st, pfx + "wb%d" % i, [128, 16, 512], BF16) for i in range(2)]
    wb_b = [K.buf() for _ in range(2)]
    w_out = P["w_out"]
    wctr = [0]

    def load_wo(ci):
        s = wctr[0] % 2
        wctr[0] += 1
        self.wload("Bwb%d" % s, wb[s][:], w_out[:, ci * 512:(ci + 1) * 512].rearrange("(k p) c -> p k c", p=128), wb_b[s])
        return s

    pb = [0]
    KEND = int(os.environ.get("KEND", "9"))
    s_next = load_wo(0) if KEND >= 3 else 0
    for ci in range(4 if KEND >= 3 else 0):
        s = s_next
        if ci + 1 < 4:
            s_next = load_wo(ci + 1)
        for tb in range(NB):
            bi = pb[0] % 4
            pb[0] += 1
            ps, ps_b = self.bank[bi], self.bbuf[bi]
            for kc in range(16):
                K.op("pe", lambda e, kc=kc: e.matmul(ps[:, :], lhsT=mixT[:, kc, tb * 128:(tb + 1) * 128], rhs=wb[s][:, kc, :],
                                                     start=(kc == 0), stop=(kc == 15)),
                     reads=[mixT_b, wb_b[s]], writes=[ps_b] if kc == 0 else (), pwrites=() if kc == 0 else [ps_b], inc=(kc == 15))
            K.op("dve", lambda e: e.tensor_tensor(out=xs[:, tb, ci * 512:(ci + 1) * 512], in0=xs[:, tb, ci * 512:(ci + 1) * 512],
                                                  in1=ps[:, :], op=ALU.add), reads=[ps_b, xs_b], pwrites=[xs_b])

    g_ffn, g_ffn_b = self.load_bcast(st, pfx + "gffn", P["norm_ffn"], D, key="gmix")
    hT = mixT
    hT_b = mixT_b
    K.wait_all("act", [hT_b])
    K.wait_all("dve", [hT_b])
    with ExitStack() as st2:
        K.scope()
        self.rms_to_hT(xs, xs_b, g_ffn, g_ffn_b, hT, hT_b, st2)
        K.end_scope()
    wd = [self.sb(st, pfx + "wd%d" % i, [128, 4, D], BF16) for i in range(2)]
    wd_b = [K.buf() for _ in range(2)]
    aT = [self.sb(st, pfx + "aT%d" % i, [128, 4, NT], BF16) for i in range(2)]
    aT_b = [K.buf() for _ in range(2)]
    rl = [self.sb(st, pfx + "rl%d" % i, [128, 512], BF16) for i in range(2)]
    rl_b = [K.buf() for _ in range(2)]
    w_up, w_dn = P["w_ff_up"], P["w_ff_down"]
    NG = DFF // 512

    def load_ffn(gi):
        s = wctr[0] % 2
        wctr[0] += 1
        self.wload("Bwb%d" % s, wb[s][:], w_up[:, gi * 512:(gi + 1) * 512].rearrange("(k p) c -> p k c", p=128), wb_b[s])
        sd = gi % 2
        self.wload("Bwd%d" % sd, wd[sd][:], w_dn[gi * 512:(gi + 1) * 512, :].rearrange("(k p) c -> p k c", p=128), wd_b[sd])
        return s, sd

    if KEND < 4:
        NG = 0
    nxt = load_ffn(0) if NG else None
    rc = [0]
    for gi in range(NG):
        s, sd = nxt
        if gi + 1 < NG:
            nxt = load_ffn(gi + 1)
        a = gi % 2
        K.wait_all("pool", [aT_b[a]])
        for fc in range(4):
            for tt in range(2):
                bi = pb[0] % 4
                pb[0] += 1
                ps, ps_b = self.bank[bi], self.bbuf[bi]
                for kc in range(16):
                    K.op("pe", lambda e, kc=kc: e.matmul(ps[:, :], lhsT=wb[s][:, kc, fc * 128:(fc + 1) * 128],
                                                         rhs=hT[:, kc, tt * 512:(tt + 1) * 512], start=(kc == 0), stop=(kc == 15)),
                         reads=[hT_b, wb_b[s]], writes=[ps_b] if kc == 0 else (), pwrites=() if kc == 0 else [ps_b], inc=(kc == 15))
                ri = rc[0] % 2
                rc[0] += 1
                K.op("act", lambda e: e.activation(out=rl[ri][:], in_=ps[:, :], func=AF.Relu), reads=[ps_b], writes=[rl_b[ri]])
                K.op("pool", lambda e: e.tensor_tensor(out=aT[a][:, fc, tt * 512:(tt + 1) * 512], in0=rl[ri][:], in1=rl[ri][:], op=ALU.mult),
                     reads=[rl_b[ri]], pwrites=[aT_b[a]])
        for tb in range(NB):
            for ct in range(4):
                bi = 4 + pb[0] % 4
                pb[0] += 1
                ps, ps_b = self.bank[bi], self.bbuf[bi]
                for fc in range(4):
                    K.op("pe", lambda e, fc=fc: e.matmul(ps[:, :], lhsT=aT[a][:, fc, tb * 128:(tb + 1) * 128],
                                                         rhs=wd[sd][:, fc, ct * 512:(ct + 1) * 512], start=(fc == 0), stop=(fc == 3)),
                         reads=[aT_b[a], wd_b[sd]], writes=[ps_b] if fc == 0 else (), pwrites=() if fc == 0 else [ps_b], inc=(fc == 3))
                K.op("dve", lambda e: e.tensor_tensor(out=xs[:, tb, ct * 512:(ct + 1) * 512], in0=xs[:, tb, ct * 512:(ct + 1) * 512],
                                                      in1=ps[:, :], op=ALU.add), reads=[ps_b, xs_b], pwrites=[xs_b])
    return st, xs, xs_b


Prog.phase_B = phase_B


PARAM_SHAPES = {
    "norm_mix": [D], "w_in": [D, IN_COLS], "diff_q_norm": [64], "diff_k_norm": [64],
    "diff_lambda_q1": [64], "diff_lambda_k1": [64], "diff_lambda_q2": [64], "diff_lambda_k2": [64],
    "diff_subln": [128], "fox_q_norm": [128], "fox_k_norm": [128], "fox_forget_bias": [6],
    "mla_q_a_norm": [512], "mla_kv_a_norm": [256], "mla_w_q_up": [512, 1152], "mla_w_kv_up": [256, 1536],
    "mla_q_norm": [192], "mla_k_norm": [192], "w_out": [D, D], "norm_ffn": [D],
    "w_ff_up": [D, DFF], "w_ff_down": [DFF, D],
}
A_PARAMS = ["norm_mix", "w_in", "diff_q_norm", "diff_k_norm", "fox_q_norm", "fox_k_norm", "fox_forget_bias",
            "mla_q_a_norm", "mla_kv_a_norm", "mla_w_q_up", "mla_w_kv_up", "mla_q_norm", "mla_k_norm"]
B_PARAMS = ["diff_lambda_q1", "diff_lambda_k1", "diff_lambda_q2", "diff_lambda_k2", "diff_subln", "w_out", "norm_ffn",
            "w_ff_up", "w_ff_down"]


def _rope_tables(pos, rot_dim):
    half = rot_dim // 2
    inv = (np.float32(500000.0) ** (-np.arange(half, dtype=np.float32) / np.float32(half))).astype(np.float32)
    ang = pos.astype(np.float32)[:, None] * inv[None, :]
    return np.cos(ang).astype(np.float32), np.sin(ang).astype(np.float32)


def core_constants(c):
    pos = np.concatenate([np.arange(128) + (8 * j + c) * 128 for j in range(NB)])
    cP, sP = _rope_tables(pos, 16)
    cM, sM = _rope_tables(pos, 64)
    mask = np.zeros((128, 8, 128), np.float32)
    si = np.arange(128)[:, None]
    ti = np.arange(128)[None, :]
    for r in range(8):
        if r > c:
            mask[:, r, :] = NEG
        elif r == c:
            mask[:, r, :] = np.where(si > ti, NEG, 0.0)
    onehot = np.zeros((128, 8), np.float32)
    onehot[:, c] = 1.0
    tri = (si <= ti).astype(np.float32)
    gi = np.arange(64)
    sl = (gi[:, None] < gi[None, :]).astype(np.float32)
    return {
        "c_cosP": cP, "c_sinP": sP, "c_cosM": cM, "c_sinM": sM,
        "c_mask": mask.astype(ml_dtypes.bfloat16), "c_onehot": onehot, "c_tri": tri, "c_sl": sl,
        "c_ident": np.eye(128, dtype=np.float32).astype(ml_dtypes.bfloat16),
    }


def build_A():
    p = Prog("A")
    p.setup_common()
    for nm, shp, dt in (("c_cosP", [NT, 8], F32), ("c_sinP", [NT, 8], F32), ("c_cosM", [NT, 32], F32), ("c_sinM", [NT, 32], F32)):
        p.din(nm, shp, dt)
    P = {nm: p.din(nm, PARAM_SHAPES[nm], F32) for nm in A_PARAMS}
    x = p.din("x", [NT, D], F32)
    dQT = p.dout("QT", [NQT * 128, NT], BF16)
    dKT = p.dout("KT", [NKT * 128, NT], BF16)
    dV = p.dout("V", [NH * NT, 128], BF16)
    dLF = p.dout("LF", [NT, 6], F32)
    def get_xs(st):
        xs = p.sb(st, "xs", [128, NB, D], F32)
        xs_b = p.K.buf("xs")
        p.K.dma("sp", "xs", xs[:], x.rearrange("(j i) d -> i j d", i=128), writes=[xs_b])
        return xs, xs_b

    p.phase_A(0, get_xs, P, dQT, dKT, dV, dLF)
    p.es.close()
    return p.nc


def build_B():
    p = Prog("B")
    p.setup_common()
    for nm, shp, dt in (("c_mask", [128, 8, 128], BF16), ("c_onehot", [128, 8], F32), ("c_tri", [128, 128], F32),
                        ("c_sl", [64, 64], F32)):
        p.din(nm, shp, dt)
    P = {nm: p.din(nm, PARAM_SHAPES[nm], F32) for nm in B_PARAMS}
    lam = p.din("lam_init", [1], F32)
    x = p.din("x", [NT, D], F32)
    dQT = p.din("QT", [NQT * 128, NT], BF16)
    dKTg = p.din("KTg", [NCORES * NKT * 128, NT], BF16)
    dVg = p.din("Vg", [NCORES * NH * NT, 128], BF16)
    dLFg = p.din("LFg", [NCORES * NT, 6], F32)
    dCR = p.dint("CR", [144, 128], BF16)
    y = p.dout("y", [NT, D], F32)

    def get_xs(st):
        xs = p.sb(st, "xs", [128, NB, D], F32)
        xs_b = p.K.buf("xs")
        p.K.dma("sp", "xs", xs[:], x.rearrange("(j i) d -> i j d", i=128), writes=[xs_b])
        return xs, xs_b

    st, xs, xs_b = p.phase_B(0, P, dQT, dKTg, dVg, dLFg, dCR, get_xs, lam)
    yb = p.dbufs["y"]
    p.K.dma("sp", "yout", y.rearrange("(j i) d -> i j d", i=128), xs[:], reads=[xs_b], writes=[yb])
    p.K.wait_all("sp", [yb])
    p.K.end_scope()
    st.close()
    p.es.close()
    return p.nc


_CACHE = {}


def _get(name, fn):
    if name not in _CACHE:
        _CACHE[name] = fn()
    return _CACHE[name]


def shard_tokens(x2d, c):
    return np.ascontiguousarray(x2d.reshape(NB, NCORES, 128, -1)[:, c].reshape(NT, -1))


def unshard_tokens(parts):
    out = np.empty((NB, NCORES, 128, parts[0].shape[-1]), parts[0].dtype)
    for c in range(NCORES):
        out[:, c] = parts[c].reshape(NB, 128, -1)
    return out.reshape(S, -1)


def kernel(**inp):
    x = np.asarray(inp["x"], np.float32).reshape(S, D)
    consts = [core_constants(c) for c in range(NCORES)]
    xc = [shard_tokens(x, c) for c in range(NCORES)]
    ncA = _get("A", build_A)
    ncB = _get("B", build_B)
    a_c = ("c_cosP", "c_sinP", "c_cosM", "c_sinM", "c_ident")
    b_c = ("c_mask", "c_onehot", "c_tri", "c_sl", "c_ident")
    for L in range(2):
        pa = {nm: np.ascontiguousarray(np.asarray(inp[nm], np.float32)[L]) for nm in A_PARAMS}
        pb = {nm: np.ascontiguousarray(np.asarray(inp[nm], np.float32)[L]) for nm in B_PARAMS}
        in_maps = []
        for c in range(NCORES):
            m = dict(pa)
            m.update({k: consts[c][k] for k in a_c})
            m["x"] = xc[c]
            in_maps.append(m)
        ra = run_bass_kernel_spmd(ncA, in_maps, core_ids=list(range(NCORES))).results
        KTg = np.concatenate([ra[c]["KT"] for c in range(NCORES)], axis=0)
        Vg = np.concatenate([ra[c]["V"] for c in range(NCORES)], axis=0)
        LFg = np.concatenate([ra[c]["LF"] for c in range(NCORES)], axis=0)
        lam_init = np.array([0.8 - 0.6 * math.exp(-0.3 * L)], np.float32)
        in_maps = []
        for c in range(NCORES):
            m = dict(pb)
            m.update({k: consts[c][k] for k in b_c})
            m.update({"x": xc[c], "QT": ra[c]["QT"], "KTg": KTg, "Vg": Vg, "LFg": LFg, "lam_init": lam_init})
            in_maps.append(m)
        rb = run_bass_kernel_spmd(ncB, in_maps, core_ids=list(range(NCORES))).results
        xc = [rb[c]["y"] for c in range(NCORES)]
    return unshard_tokens(xc).reshape(1, S, D).astype(np.float32)
```
